# Optimizing a Trainium2 kernel written in Bass

```python
import math
import jax, jax.numpy as jnp
from jax import lax
import numpy as np

D_MODEL = 1024
BATCH = 2
SEQ = 16384
DEPTH = 4

GRID_W = 64
CTX_LEN = 256

N_ATT_HEADS = 4
ATT_HEAD_DIM = 64
ATT_V_DIM = 2 * ATT_HEAD_DIM
QK_WIDTH = N_ATT_HEADS * 2 * ATT_HEAD_DIM
ATT_WIDTH = N_ATT_HEADS * ATT_V_DIM
CONV_WIDTH = D_MODEL // 4
CONV_WIDTH_K = 3
FOURIER_WIDTH = D_MODEL // 4
FOURIER_GROUPS = 4
FOURIER_GROUP_DIM = FOURIER_WIDTH // FOURIER_GROUPS

MIX_WIDTH = ATT_WIDTH + CONV_WIDTH + FOURIER_WIDTH
IN_WIDTH = 2 * QK_WIDTH + ATT_WIDTH + 3 * CONV_WIDTH + FOURIER_WIDTH
SPLITS = (QK_WIDTH, 2 * QK_WIDTH, 2 * QK_WIDTH + ATT_WIDTH,
          2 * QK_WIDTH + ATT_WIDTH + CONV_WIDTH,
          2 * QK_WIDTH + ATT_WIDTH + 2 * CONV_WIDTH,
          2 * QK_WIDTH + ATT_WIDTH + 3 * CONV_WIDTH)

D_FF = 4 * D_MODEL
Q_BLOCK = 128
ROPE_BASE = 10000.0
LN_EPS = 1e-5
RMS_EPS = 1e-5
DEEPNORM_ALPHA = (2 * DEPTH) ** 0.25
DEEPNORM_BETA = (8 * DEPTH) ** -0.25

kernel_name = "hymba_diffattn_conv_fnet_deepnorm_trunk"


def layer_norm(x, g, b):
    xf = x.astype(jnp.float32)
    mu = jnp.mean(xf, axis=-1, keepdims=True)
    var = jnp.mean(jnp.square(xf - mu), axis=-1, keepdims=True)
    y = (xf - mu) * lax.rsqrt(var + LN_EPS)
    return (y * g.astype(jnp.float32) + b.astype(jnp.float32)).astype(x.dtype)


def rms_norm(x, w):
    xf = x.astype(jnp.float32)
    y = xf * lax.rsqrt(jnp.mean(jnp.square(xf), axis=-1, keepdims=True) + RMS_EPS)
    return (y * w.astype(jnp.float32)).astype(x.dtype)


def modulate(x, shift, scale):
    return x * (1.0 + scale) + shift


def axial_rope(x, rows, cols):
    half = ATT_HEAD_DIM // 2
    quarter = half // 2
    inv_freq = 1.0 / (ROPE_BASE ** (jnp.arange(0, half, 2, dtype=jnp.float32) / half))

    def rot(xa, pos):
        ang = pos.astype(jnp.float32)[:, None] * inv_freq[None, :]
        cos = jnp.cos(ang)[None, :, None, None, :].astype(xa.dtype)
        sin = jnp.sin(ang)[None, :, None, None, :].astype(xa.dtype)
        x1, x2 = xa[..., :quarter], xa[..., quarter:]
        return jnp.concatenate([x1 * cos - x2 * sin, x2 * cos + x1 * sin], axis=-1)

    return jnp.concatenate([rot(x[..., :half], rows), rot(x[..., half:], cols)], axis=-1)


def split_proj(p):
    B, L, _ = p.shape
    q, k, v, u, gb, gc, f = jnp.split(p, SPLITS, axis=-1)
    q = q.reshape(B, L, N_ATT_HEADS, 2, ATT_HEAD_DIM)
    k = k.reshape(B, L, N_ATT_HEADS, 2, ATT_HEAD_DIM)
    v = v.reshape(B, L, N_ATT_HEADS, ATT_V_DIM)
    return q, k, v, u, gb, gc, f


def diff_attention(q, k, v, lam):
    s = jnp.einsum('bqhpd,bkhpd->bhpqk', q, k).astype(jnp.float32) * (ATT_HEAD_DIM ** -0.5)
    p = jax.nn.softmax(s, axis=-1)
    a = p[:, :, 0] - lam * p[:, :, 1]
    return jnp.einsum('bhqk,bkhe->bqhe', a.astype(v.dtype), v)


def diff_attention_blocked(q, k, v, lam):
    B, L = q.shape[0], q.shape[1]
    nb = L // Q_BLOCK
    qb = q.reshape(B, nb, Q_BLOCK, N_ATT_HEADS, 2, ATT_HEAD_DIM).transpose(1, 0, 2, 3, 4, 5)
    out = lax.map(lambda qi: diff_attention(qi, k, v, lam), qb)
    return out.transpose(1, 0, 2, 3, 4).reshape(B, L, N_ATT_HEADS, ATT_V_DIM)


def diff_head_out(o, subln_w, lam_init):
    B, L = o.shape[0], o.shape[1]
    return (rms_norm(o, subln_w) * (1.0 - lam_init)).reshape(B, L, ATT_WIDTH)


def short_conv(u, gb, gc, w, b):
    z = gc * u
    y = lax.conv_general_dilated(z, w[:, None, :].astype(z.dtype), window_strides=(1,),
                                 padding=((1, 1),), dimension_numbers=('NWC', 'WIO', 'NWC'),
                                 feature_group_count=CONV_WIDTH)
    return gb * (y + b)


def fourier_mix(f):
    B, L, _ = f.shape
    fg = f.reshape(B, L, FOURIER_GROUPS, FOURIER_GROUP_DIM).astype(jnp.float32)
    F = jnp.fft.fftn(fg, axes=(1, 3), norm='ortho')
    return jnp.real(F).reshape(B, L, FOURIER_WIDTH).astype(f.dtype)


def squared_relu_mlp(h, w_up, w_down):
    return jnp.square(jax.nn.relu(h @ w_up)) @ w_down


def setup_inputs(seed: int = 0) -> dict:
    key = jax.random.key(seed)
    ks = jax.random.split(key, 20)
    f32 = jnp.float32
    D = D_MODEL
    x = jax.random.normal(ks[0], (BATCH, SEQ, D), f32)
    c = jax.random.normal(ks[1], (BATCH, D), f32)
    ctx = jax.random.normal(ks[2], (BATCH, CTX_LEN, D), f32)
    c_ctx = jax.random.normal(ks[3], (D,), f32)
    w_mod = jax.random.normal(ks[4], (DEPTH, D, 6 * D), f32) * D ** -0.5
    b_mod = jax.random.normal(ks[5], (DEPTH, 6 * D), f32) * 0.02
    col_scale = jnp.ones((IN_WIDTH,), f32).at[2 * QK_WIDTH:2 * QK_WIDTH + ATT_WIDTH].set(DEEPNORM_BETA)
    w_in = jax.random.normal(ks[6], (DEPTH, D, IN_WIDTH), f32) * D ** -0.5 * col_scale
    diff_lambda = jax.random.normal(ks[7], (DEPTH, 4, ATT_HEAD_DIM), f32) * 0.1
    subln_w = 1.0 + 0.02 * jax.random.normal(ks[8], (DEPTH, ATT_V_DIM), f32)
    conv_w = jax.random.normal(ks[9], (DEPTH, CONV_WIDTH_K, CONV_WIDTH), f32) * CONV_WIDTH_K ** -0.5
    conv_b = jax.random.normal(ks[10], (DEPTH, CONV_WIDTH), f32) * 0.02
    w_out = jax.random.normal(ks[11], (DEPTH, MIX_WIDTH, D), f32) * MIX_WIDTH ** -0.5 * DEEPNORM_BETA
    ln1_g = 1.0 + 0.02 * jax.random.normal(ks[12], (DEPTH, D), f32)
    ln1_b = 0.02 * jax.random.normal(ks[13], (DEPTH, D), f32)
    w_up = jax.random.normal(ks[14], (DEPTH, D, D_FF), f32) * D ** -0.5 * DEEPNORM_BETA
    w_down = jax.random.normal(ks[15], (DEPTH, D_FF, D), f32) * D_FF ** -0.5 * DEEPNORM_BETA
    ln2_g = 1.0 + 0.02 * jax.random.normal(ks[16], (DEPTH, D), f32)
    ln2_b = 0.02 * jax.random.normal(ks[17], (DEPTH, D), f32)
    return {"x": x, "c": c, "ctx": ctx, "c_ctx": c_ctx, "w_mod": w_mod, "b_mod": b_mod,
            "w_in": w_in, "diff_lambda": diff_lambda, "subln_w": subln_w, "conv_w": conv_w,
            "conv_b": conv_b, "w_out": w_out, "ln1_g": ln1_g, "ln1_b": ln1_b, "w_up": w_up,
            "w_down": w_down, "ln2_g": ln2_g, "ln2_b": ln2_b}


def reference(x, c, ctx, c_ctx, w_mod, b_mod, w_in, diff_lambda, subln_w, conv_w, conv_b,
              w_out, ln1_g, ln1_b, w_up, w_down, ln2_g, ln2_b):
    B, S, D = x.shape
    ROWS = S // GRID_W
    rows = jnp.repeat(jnp.arange(ROWS), GRID_W)
    cols = jnp.tile(jnp.arange(GRID_W), ROWS)
    silu_c = jax.nn.silu(c)
    silu_cc = jax.nn.silu(c_ctx)

    for l in range(DEPTH):
        last = l == DEPTH - 1
        mod_x = (silu_c @ w_mod[l] + b_mod[l])[:, None, :]
        mod_c = (silu_cc @ w_mod[l] + b_mod[l])[None, None, :]
        sh1, sc1, g1, sh2, sc2, g2 = jnp.split(mod_x, 6, axis=-1)
        csh1, csc1, cg1, csh2, csc2, cg2 = jnp.split(mod_c, 6, axis=-1)

        lam_init = 0.8 - 0.6 * math.exp(-0.3 * l)
        lam_p = diff_lambda[l].astype(jnp.float32)
        lam = (jnp.exp(jnp.sum(lam_p[0] * lam_p[1])) - jnp.exp(jnp.sum(lam_p[2] * lam_p[3]))
               + lam_init)

        px = modulate(x, sh1, sc1) @ w_in[l]
        pc = modulate(ctx, csh1, csc1) @ w_in[l]
        qx, kx, vx, ux, gbx, gcx, fx = split_proj(px)
        qc, kc, vc, uc, gbc, gcc, fc = split_proj(pc)
        qx = axial_rope(qx, rows, cols)
        kx = axial_rope(kx, rows, cols)
        k_all = jnp.concatenate([kc, kx], axis=1)
        v_all = jnp.concatenate([vc, vx], axis=1)

        ox = diff_attention_blocked(qx, k_all, v_all, lam)
        mix_x = jnp.concatenate([
            diff_head_out(ox, subln_w[l], lam_init),
            short_conv(ux, gbx, gcx, conv_w[l], conv_b[l]),
            fourier_mix(fx)], axis=-1) @ w_out[l]
        x = layer_norm(DEEPNORM_ALPHA * x + g1 * mix_x, ln1_g[l], ln1_b[l])

        x = layer_norm(DEEPNORM_ALPHA * x + g2 * squared_relu_mlp(modulate(x, sh2, sc2), w_up[l], w_down[l]),
                       ln2_g[l], ln2_b[l])

        if not last:
            oc = diff_attention(qc, kc, vc, lam)
            mix_c = jnp.concatenate([
                diff_head_out(oc, subln_w[l], lam_init),
                short_conv(uc, gbc, gcc, conv_w[l], conv_b[l]),
                fourier_mix(fc)], axis=-1) @ w_out[l]
            ctx = layer_norm(DEEPNORM_ALPHA * ctx + cg1 * mix_c, ln1_g[l], ln1_b[l])
            ctx = layer_norm(DEEPNORM_ALPHA * ctx + cg2 * squared_relu_mlp(modulate(ctx, csh2, csc2), w_up[l], w_down[l]),
                             ln2_g[l], ln2_b[l])
    return x
```

```python
import math
from contextlib import ExitStack
import numpy as np
import ml_dtypes
import concourse.bass as bass
import concourse.mybir as mybir
from concourse.bass_utils import run_bass_kernel_spmd

F32 = mybir.dt.float32
BF16 = mybir.dt.bfloat16
ALU = mybir.AluOpType
AF = mybir.ActivationFunctionType
AX = mybir.AxisListType

NCORE = 8
NL = 4
D = 1024
LAT = 2048
NTOK = 4608
SEQ = 16384
NKT = 130
ALPHA = float(8 ** 0.25)
EPS = 1e-5
ENGS = ("tensor", "vector", "scalar", "gpsimd", "sync")


class Buf:
    __slots__ = ("w", "r", "dsem")

    def __init__(self):
        self.w = {}
        self.r = {}
        self.dsem = None


class Sem:
    def __init__(self, h, i):
        self.h = h
        self.id = i
        self.count = 0


class Trk:
    def __init__(self, nc, stack):
        self.nc = nc
        self.P = {}
        n = 0
        for e in ENGS:
            self.P[e] = Sem(stack.enter_context(nc.semaphore("P_" + e)), n)
            n += 1
        self.pool = []
        for i in range(80):
            self.pool.append(Sem(stack.enter_context(nc.semaphore("dq%d" % i)), n))
            n += 1
        self.cc = Sem(stack.enter_context(nc.semaphore("ccs")), n)
        self.free = list(self.pool)
        self.used = []
        self.seen = {e: {} for e in ENGS}
        self.q = {e: [] for e in ENGS}

    def newbuf(self):
        return Buf()

    def op(self, eng, fn, R=(), W=(), dma=None, cc=False):
        deps = {}

        def add(d):
            for sid, (s, v, se) in d.items():
                if se == "tensor" and eng == "tensor":
                    continue
                if sid not in deps or deps[sid][1] < v:
                    deps[sid] = (s, v)

        for b in R:
            add(b.w)
        for b in W:
            add(b.w)
            add(b.r)
        waits = []
        seen = self.seen[eng]
        for sid, (s, v) in deps.items():
            if seen.get(sid, 0) >= v:
                continue
            seen[sid] = v
            waits.append((s.h, v))
        if cc:
            s = self.cc
            s.count += 1
            tok = (s, s.count, "cc")
            inc = (s.h, 1)
        elif dma is not None:
            if dma.dsem is None:
                dma.dsem = self.free.pop()
                self.used.append(dma)
            s = dma.dsem
            s.count += 16
            tok = (s, s.count, "dma")
            inc = (s.h, 16)
        else:
            s = self.P[eng]
            s.count += 1
            tok = (s, s.count, eng)
            inc = (s.h, 1)
        for b in W:
            b.w = {tok[0].id: tok}
            b.r = {}
        for b in R:
            b.r[tok[0].id] = tok
        self.q[eng].append((waits, fn, inc))

    def drain(self):
        waits = []
        seen = self.seen["sync"]
        sems = [b.dsem for b in self.used] + [self.P[e] for e in ENGS if e != "sync"] + [self.cc]
        for s in sems:
            if s.count > 0 and seen.get(s.id, 0) < s.count:
                seen[s.id] = s.count
                waits.append((s.h, s.count))
        self.q["sync"].append((waits, None, None))
        for b in self.used:
            self.free.append(b.dsem)
            b.dsem = None
        self.used = []

    def emit(self):
        with self.nc.Block() as blk:
            for en in ENGS:
                ops = self.q[en]
                self.q[en] = []
                if not ops:
                    continue

                def mk(ops):
                    def f(e):
                        for waits, fn, inc in ops:
                            for s, v in waits:
                                e.wait_ge(s, v)
                            if fn is not None:
                                ins = fn(e)
                                ins.then_inc(inc[0], inc[1])
                    return f

                getattr(blk, en)(mk(ops))


class Builder:
    def __init__(self, nlayers=NL, dbg=False):
        self.nl = nlayers
        self.dbg = dbg
        self.nc = bass.Bass("TRN2", target_bir_lowering=False)

    def din(self, name, shape, dt=F32):
        return self.nc.dram_tensor(name, list(shape), dt, kind="ExternalInput").ap()

    def dout(self, name, shape, dt=F32):
        return self.nc.dram_tensor(name, list(shape), dt, kind="ExternalOutput").ap()

    def dscr(self, name, shape, dt=F32):
        return self.nc.dram_tensor(name, list(shape), dt).ap()

    def sb(self, name, shape, dt=F32):
        self._n += 1
        return self.st.enter_context(self.nc.sbuf_tensor("%s_%d" % (name, self._n), list(shape), dt))

    def sbn(self, name, shape, dt, n):
        return [(self.sb(name, shape, dt), Buf()) for _ in range(n)]

    def dma(self, q, out, in_, buf, R=(), W=(), slow=False):
        if slow:
            fn = lambda e, o=out, i=in_: e.dma_start(out=o, in_=i, allow_slow_non_contiguous=True)
        else:
            fn = lambda e, o=out, i=in_: e.dma_start(out=o, in_=i)
        self.t.op(q, fn, R=R, W=W, dma=buf)

    def mm(self, out, lhsT, rhs, start, stop, R, W):
        self.t.op("tensor", lambda e, o=out, l=lhsT, r=rhs, s=start, p=stop: e.matmul(o, l, r, start=s, stop=p), R=R, W=W)

    def tr(self, out, in_, ident, R, W):
        self.t.op("tensor", lambda e, o=out, i=in_, d=ident: e.transpose(o, i, d), R=R, W=W)

    def act(self, out, in_, func, R, W, bias=None, scale=None, eng="scalar"):
        kw = {}
        if bias is not None:
            kw["bias"] = bias
        if scale is not None:
            kw["scale"] = scale
        self.t.op(eng, lambda e, o=out, i=in_, f=func, kw=kw: e.activation(out=o, in_=i, func=f, **kw), R=R, W=W)

    def tt(self, eng, out, in0, in1, op, R, W):
        self.t.op(eng, lambda e, o=out, a=in0, b=in1, p=op: e.tensor_tensor(out=o, in0=a, in1=b, op=p), R=R, W=W)

    def ts(self, eng, out, in0, s1, s2, op0, op1, R, W):
        if op1 is None:
            self.t.op(eng, lambda e, o=out, a=in0, x=s1, p=op0: e.tensor_scalar(out=o, in0=a, scalar1=x, scalar2=None, op0=p), R=R, W=W)
        else:
            self.t.op(eng, lambda e, o=out, a=in0, x=s1, y=s2, p=op0, q=op1: e.tensor_scalar(out=o, in0=a, scalar1=x, scalar2=y, op0=p, op1=q), R=R, W=W)

    def stt(self, out, in0, scalar, in1, op0, op1, R, W):
        self.t.op("vector", lambda e, o=out, a=in0, s=scalar, b=in1, p=op0, q=op1: e.scalar_tensor_tensor(out=o, in0=a, scalar=s, in1=b, op0=p, op1=q), R=R, W=W)

    def cp(self, eng, out, in_, R, W):
        self.t.op(eng, lambda e, o=out, i=in_: e.tensor_copy(out=o, in_=i), R=R, W=W)

    def phase(self, fn, *a):
        with ExitStack() as st:
            self.st = st
            fn(*a)
            self.t.drain()
            self.t.emit()
        self.st = self.gst

    def load_w(self, wbf, wB, src, K, N, ncol):
        stg = self.sbn("wstg", [128, ncol], F32, 3)
        i = 0
        for k in range(K):
            for c0 in range(0, N, ncol):
                w = min(ncol, N - c0)
                stt, sB = stg[i % 3]
                self.dma(("sync", "gpsimd")[i % 2], stt[:, 0:w], src[k * 128:(k + 1) * 128, c0:c0 + w], sB, W=[sB])
                self.cp(("gpsimd", "vector")[i % 2], wbf[:, k, c0:c0 + w], stt[:, 0:w], R=[sB], W=[wB[k]])
                i += 1

    def build(self):
        nc = self.nc
        nl = self.nl
        I = {}
        I["xin"] = self.din("xin", [NTOK, D])
        I["ccT"] = self.din("ccT", [128, 8, 3])
        I["w_mod"] = self.din("w_mod", [NL, D, 6 * D])
        I["b_mod"] = self.din("b_mod", [NL, 6 * D])
        I["w_in_fm"] = self.din("w_in_fm", [NL, D, 2816])
        I["w_in_tm"] = self.din("w_in_tm", [NL, D, 768])
        I["diff_lambda"] = self.din("diff_lambda", [NL, 256])
        I["subln_w"] = self.din("subln_w", [NL, 128])
        I["conv_w"] = self.din("conv_w", [NL, 3, 256])
        I["conv_b"] = self.din("conv_b", [NL, 256])
        I["w_out"] = self.din("w_out", [NL, D, D])
        I["ln1_g"] = self.din("ln1_g", [NL, D])
        I["ln1_b"] = self.din("ln1_b", [NL, D])
        I["w_up"] = self.din("w_up", [NL, D, 4 * D])
        I["w_down"] = self.din("w_down", [NL, 4 * D, D])
        I["ln2_g"] = self.din("ln2_g", [NL, D])
        I["ln2_b"] = self.din("ln2_b", [NL, D])
        I["ropeC"] = self.din("ropeC", [128, NTOK])
        I["ropeS"] = self.din("ropeS", [128, NTOK])
        I["ident"] = self.din("ident", [128, 128])
        I["Ec"] = self.din("Ec", [128, 128, 128], BF16)
        I["Es"] = self.din("Es", [128, 128, 128], BF16)
        I["T2a"] = self.din("T2a", [128, 32], BF16)
        I["T2b"] = self.din("T2b", [128, 32], BF16)
        I["CD"] = self.din("CD", [128, 128], BF16)
        I["nSD"] = self.din("nSD", [128, 128], BF16)
        I["C256"] = self.din("C256", [128, 2, 256], BF16)
        I["S256"] = self.din("S256", [128, 2, 256], BF16)
        I["selL"] = self.din("selL", [128, 8])
        I["selR"] = self.din("selR", [128, 8])
        self.I = I
        self.yout = self.dout("y", [2 * LAT, D])
        S = {}
        S["xres"] = self.dscr("xres", [NTOK, D])
        S["x1"] = self.dscr("x1", [NTOK, D])
        S["gates"] = self.dscr("gates", [NL, 2, 3, D])
        S["qT"] = self.dscr("qT", [4, 128, NTOK], BF16)
        S["kTc"] = self.dscr("kTc", [2, 4, 128, 256], BF16)
        S["vc"] = self.dscr("vc", [512, 512], BF16)
        S["fc"] = self.dscr("fc", [512, 256], BF16)
        S["z"] = self.dscr("z", [2, 128, NTOK], BF16)
        S["gb"] = self.dscr("gb", [2, 128, NTOK], BF16)
        S["mixT"] = self.dscr("mixT", [8, 128, NTOK], BF16)
        S["hT"] = self.dscr("hT", [9, 128, 32, 512], BF16)
        S["a1"] = self.dscr("a1", [2, 128, 128, 512], BF16)
        S["eKT"] = self.dscr("eKT", [1024, LAT], BF16)
        S["eV"] = self.dscr("eV", [2 * LAT, 512], BF16)
        S["eF"] = self.dscr("eF", [2 * LAT, 256], BF16)
        S["eH"] = self.dscr("eH", [8, 128], F32)
        S["gKT"] = self.dscr("gKT", [8 * 1024, LAT], BF16)
        S["gV"] = self.dscr("gV", [8 * 2 * LAT, 512], BF16)
        S["gF"] = self.dscr("gF", [8 * 2 * LAT, 256], BF16)
        S["gH"] = self.dscr("gH", [64, 128], F32)
        self.S = S
        if self.dbg:
            self.dbgout = {k: self.dout("dbg_" + k, S[k].shape, S[k].dtype) for k in self.dbg}

        self._n = 0
        with ExitStack() as gst:
            self.gst = gst
            self.st = gst
            self.t = Trk(nc, gst)
            self.ident = self.sb("ident", [128, 128], F32)
            self.identB = Buf()
            self.ones = self.sb("ones", [128, 128], F32)
            self.ones128 = self.sb("ones128", [128, 128], F32)
            self.onesB = Buf()
            self.epsT = self.sb("epsT", [128, 1], F32)
            self.modT = self.sb("modT", [128, NL, 144], F32)
            self.modB = Buf()
            self.PSB = [gst.enter_context(nc.psum_tensor("ps%d" % i, [128, 1024], F32)) for i in range(4)]
            self.PB = [[Buf(), Buf()] for _ in range(4)]
            self.phase(self.prologue)
            for l in range(nl):
                last = (l == NL - 1)
                self.phase(self.phaseA, l)
                self.phase(self.phaseAG)
                self.phase(self.phaseC, l, last)
                self.phase(self.phaseF, l, last)
                self.phase(self.phaseF2, l, last)
                self.phase(self.phaseB, l, last)
                self.phase(self.phaseD, l, last)
                self.phase(self.phaseE1, l, last)
                self.phase(self.phaseE2, l, last, l == nl - 1)
            if self.dbg:
                self.phase(self.phaseDbg)
        return nc

    def ps(self, i, h):
        return self.PSB[i][:, h * 512:(h + 1) * 512], self.PB[i][h]

    def phaseDbg(self):
        for k in self.dbg:
            b = Buf()
            src = self.S[k]
            dst = self.dbgout[k]
            self.dma("sync", dst, src, b, W=[b])

    def prologue(self):
        I = self.I
        self.dma("sync", self.ident[:], I["ident"], self.identB, W=[self.identB])
        self.t.op("vector", lambda e: e.memset(self.ones[:], 1.0), W=[self.onesB])
        self.t.op("vector", lambda e: e.memset(self.ones128[:], 1.0 / 128.0), W=[self.onesB])
        self.t.op("vector", lambda e: e.memset(self.epsT[:], EPS), W=[self.onesB])
        ccT = self.sb("ccT", [128, 8, 3])
        scT = self.sb("scT", [128, 8, 3])
        ccB = Buf()
        scB = Buf()
        self.dma("sync", ccT[:], I["ccT"], ccB, W=[ccB])
        self.act(scT[:], ccT[:], AF.Silu, R=[ccB], W=[scB])
        bm3 = self.sb("bm3", [3, 6144])
        bmB = Buf()
        modrow = self.sb("modrow", [3, 6144])
        mrB = Buf()
        wm = self.sbn("wm", [128, 8, 512], F32, 2)
        it = 0
        for l in range(self.nl):
            self.dma("gpsimd", bm3[:], I["b_mod"][l, :].partition_broadcast(3), bmB, W=[bmB])
            for ncn in range(12):
                w, wB = wm[it % 2]
                src = I["w_mod"][l].rearrange("(k p) n -> p k n", p=128)[:, :, ncn * 512:(ncn + 1) * 512]
                self.dma(("sync", "gpsimd")[it % 2], w[:], src, wB, W=[wB])
                pt, pB = self.ps(it % 2, 0)
                for k in range(8):
                    self.mm(pt[0:3, :], scT[:, k, :], w[:, k, :], k == 0, k == 7, R=[scB, wB], W=[pB])
                self.tt("vector", modrow[0:3, ncn * 512:(ncn + 1) * 512], pt[0:3, :], bm3[0:3, ncn * 512:(ncn + 1) * 512], ALU.add, R=[pB, bmB], W=[mrB])
                it += 1
            for c0 in (1024, 4096):
                self.ts("vector", modrow[0:3, c0:c0 + 1024], modrow[0:3, c0:c0 + 1024], 1.0, None, ALU.add, None, R=[mrB], W=[mrB])
            self.dma("sync", self.S["gates"][l, 0], modrow[0:3, 2048:3072], mrB, R=[mrB])
            self.dma("sync", self.S["gates"][l, 1], modrow[0:3, 5120:6144], mrB, R=[mrB])
            pt, pB = self.ps(2, 0)
            for c in range(48):
                self.tr(pt[:, c * 3:(c + 1) * 3], modrow[0:3, c * 128:(c + 1) * 128], self.ident[0:3, 0:3], R=[mrB, self.identB], W=[pB])
            self.cp("vector", self.modT[:, l, :], pt[:, 0:144], R=[pB], W=[self.modB])

    def mod(self, l, which, k, s):
        c = (which * 8 + k) * 3 + s
        return self.modT[:, l, c:c + 1]

    def phaseA(self, l):
        I, S = self.I, self.S
        win_fm = self.sb("win_fm", [128, 8, 2816], BF16)
        wfB = [Buf() for _ in range(8)]
        win_tm = self.sb("win_tm", [128, 8, 768], BF16)
        wtB = [Buf() for _ in range(8)]
        self.load_w(win_fm, wfB, I["w_in_fm"][l], 8, 2816, 1408)
        self.load_w(win_tm, wtB, I["w_in_tm"][l], 8, 768, 768)
        xt = self.sbn("xt", [128, 4, 1024], F32, 2)
        xmT = self.sbn("xmT", [128, 8, 512], BF16, 2)
        rc = self.sbn("rc", [128, 512], F32, 2)
        rs = self.sbn("rs", [128, 512], F32, 2)
        t1 = self.sbn("t1", [128, 512], F32, 2)
        t2 = self.sbn("t2", [128, 512], F32, 2)
        qo = self.sbn("qo", [128, 512], BF16, 3)
        usb = self.sbn("usb", [128, 512], F32, 2)
        zo = self.sbn("zo", [128, 512], BF16, 2)
        zh = self.sbn("zh", [128, 512], F32, 2)
        gbo = self.sbn("gbo", [128, 512], BF16, 2)
        vsb = self.sbn("vsb", [128, 4, 512], BF16, 2)
        fsb = self.sbn("fsb", [128, 4, 256], BF16, 2)
        src = I["xin"] if l == 0 else S["xres"]
        eKT = S["eKT"].rearrange("(b h d) t -> b h d t", b=2, h=4)
        cnt = {"fm": 0, "q": 0, "z": 0}
        for blk in range(9):
            s = 0 if blk < 4 else (1 if blk < 8 else 2)
            tok0 = blk * 512
            x, xB = xt[blk % 2]
            self.dma("sync", x[:], src[tok0:tok0 + 512, :].rearrange("(j p) d -> p j d", p=128), xB, W=[xB])
            c_, cB = rc[blk % 2]
            s_, sB = rs[blk % 2]
            self.dma("gpsimd", c_[:], I["ropeC"][:, tok0:tok0 + 512], cB, W=[cB])
            self.dma("gpsimd", s_[:], I["ropeS"][:, tok0:tok0 + 512], sB, W=[sB])
            xm, xmB = xmT[blk % 2]
            for k in range(8):
                pt, pB = self.ps(0, k % 2)
                for j in range(4):
                    self.tr(pt[:, j * 128:(j + 1) * 128], x[:, j, k * 128:(k + 1) * 128], self.ident[:], R=[xB, self.identB], W=[pB])
                self.act(xm[:, k, :], pt, AF.Identity, R=[pB, self.modB], W=[xmB], bias=self.mod(l, 0, k, s), scale=self.mod(l, 1, k, s))

            def fm(mc, pt, pB):
                for k in range(8):
                    self.mm(pt, win_fm[:, k, mc * 128:(mc + 1) * 128], xm[:, k, :], k == 0, k == 7, R=[wfB[k], xmB], W=[pB])

            for hh in range(8):
                pa, paB = self.ps(1 + cnt["fm"] % 2, 0)
                pb, pbB = self.ps(1 + cnt["fm"] % 2, 1)
                cnt["fm"] += 1
                fm(hh, pa, paB)
                fm(14 + hh, pb, pbB)
                a, aB = t1[hh % 2]
                b, bB = t2[hh % 2]
                self.tt("vector", a[:], pa, c_[:], ALU.mult, R=[paB, cB], W=[aB])
                self.tt("vector", b[:], pb, s_[:], ALU.mult, R=[pbB, sB], W=[bB])
                o, oB = qo[cnt["q"] % 3]
                cnt["q"] += 1
                self.tt("gpsimd", o[:], a[:], b[:], ALU.add, R=[aB, bB], W=[oB])
                if hh < 4:
                    self.dma("sync", S["qT"][hh, :, tok0:tok0 + 512], o[:], oB, R=[oB])
                else:
                    h = hh - 4
                    if blk < 8:
                        bb = blk // 4
                        t0 = (blk % 4) * 512
                        self.dma("sync", eKT[bb, h, :, t0:t0 + 512], o[:], oB, R=[oB])
                    else:
                        for bb in range(2):
                            self.dma("sync", S["kTc"][bb, h], o[:, bb * 256:(bb + 1) * 256], oB, R=[oB])
            for c in range(2):
                pa, paB = self.ps(1 + cnt["fm"] % 2, 0)
                pb, pbB = self.ps(1 + cnt["fm"] % 2, 1)
                cnt["fm"] += 1
                fm(8 + c, pa, paB)
                fm(12 + c, pb, pbB)
                u, uB = usb[c]
                self.cp("vector", u[:], pa, R=[paB], W=[uB])
                zf, zfB = zh[c]
                self.tt("vector", zf[:], pb, u[:], ALU.mult, R=[pbB, uB], W=[zfB])
                z, zB = zo[c]
                self.cp("gpsimd", z[:], zf[:], R=[zfB], W=[zB])
                self.dma("sync", S["z"][c, :, tok0:tok0 + 512], z[:], zB, R=[zB])
                if blk in (0, 4):
                    wh = 0 if blk == 0 else 2
                    self.dma("gpsimd", S["eH"][c * 4 + wh, :].unsqueeze(1), zf[:, 0:1], zfB, R=[zfB], slow=True)
                if blk in (3, 7):
                    wh = 1 if blk == 3 else 3
                    self.dma("gpsimd", S["eH"][c * 4 + wh, :].unsqueeze(1), zf[:, 511:512], zfB, R=[zfB], slow=True)
                pa, paB = self.ps(1 + cnt["fm"] % 2, 0)
                cnt["fm"] += 1
                fm(10 + c, pa, paB)
                g, gB = gbo[c]
                self.act(g[:], pa, AF.Copy, R=[paB], W=[gB])
                self.dma("sync", S["gb"][c, :, tok0:tok0 + 512], g[:], gB, R=[gB])
            v, vB = vsb[blk % 2]
            f, fB = fsb[blk % 2]
            for j in range(4):
                pv, pvB = self.ps(3, 0)
                pf, pfB = self.ps(3, 1)
                for k in range(8):
                    self.mm(pv, xm[:, k, j * 128:(j + 1) * 128], win_tm[:, k, 0:512], k == 0, k == 7, R=[xmB, wtB[k]], W=[pvB])
                for k in range(8):
                    self.mm(pf[:, 0:256], xm[:, k, j * 128:(j + 1) * 128], win_tm[:, k, 512:768], k == 0, k == 7, R=[xmB, wtB[k]], W=[pfB])
                self.act(v[:, j, :], pv, AF.Copy, R=[pvB], W=[vB])
                self.cp("vector", f[:, j, :], pf[:, 0:256], R=[pfB], W=[fB])
            if blk < 8:
                r0 = (blk // 4) * LAT + (blk % 4) * 512
                self.dma("gpsimd", S["eV"][r0:r0 + 512, :].rearrange("(j p) e -> p j e", p=128), v[:], vB, R=[vB])
                self.dma("gpsimd", S["eF"][r0:r0 + 512, :].rearrange("(j p) e -> p j e", p=128), f[:], fB, R=[fB])
            else:
                self.dma("gpsimd", S["vc"].rearrange("(j p) e -> p j e", p=128), v[:], vB, R=[vB])
                self.dma("gpsimd", S["fc"].rearrange("(j p) e -> p j e", p=128), f[:], fB, R=[fB])

    def phaseAG(self):
        S = self.S
        self.gather((("eF", "gF"), ("eH", "gH")))

    def gather(self, pairs):
        S = self.S
        for a, b in pairs:
            self.t.op("gpsimd", lambda e, i=S[a], o=S[b]: e.collective_compute(
                "AllGather", ALU.bypass, replica_groups=[list(range(NCORE))], ins=[i.opt()], outs=[o.opt()]), cc=True)

    def phaseC(self, l, last):
        I, S = self.I, self.S
        cw = self.sb("cw", [128, 2, 3])
        cb = self.sb("cb", [128, 2])
        cwB = Buf()
        for k in range(3):
            self.dma("sync", cw[:, :, k], I["conv_w"][l, k, :].rearrange("(c p) -> p c", p=128), cwB, W=[cwB], slow=True)
        self.dma("sync", cb[:], I["conv_b"][l, :].rearrange("(c p) -> p c", p=128), cwB, W=[cwB], slow=True)
        selL = self.sb("selL", [128, 8])
        selR = self.sb("selR", [128, 8])
        selB = Buf()
        self.dma("gpsimd", selL[:], I["selL"], selB, W=[selB])
        self.dma("gpsimd", selR[:], I["selR"], selB, W=[selB])
        gh = self.sb("gh", [64, 128])
        ghB = Buf()
        self.dma("gpsimd", gh[:], S["gH"], ghB, W=[ghB])
        pt, pB = self.ps(0, 0)
        self.tr(pt[:, 0:64], gh[:], self.ident[0:64, 0:64], R=[ghB, self.identB], W=[pB])
        hz = self.sb("hz", [128, 8, 8])
        hzB = Buf()
        self.cp("vector", hz[:].rearrange("p r j -> p (r j)"), pt[:, 0:64], R=[pB], W=[hzB])
        halo = self.sb("halo", [128, 8])
        haB = Buf()
        tmp = self.sb("tmp8", [128, 8])
        tmB = Buf()
        for c in range(2):
            for b in range(2):
                for side in range(2):
                    wh = 2 * b + 1 if side == 0 else 2 * b
                    sel = selL if side == 0 else selR
                    self.tt("vector", tmp[:], hz[:, :, c * 4 + wh], sel[:], ALU.mult, R=[hzB, selB], W=[tmB])
                    hi = c * 4 + b * 2 + side
                    self.t.op("vector", lambda e, o=halo[:, hi:hi + 1], i=tmp[:]: e.tensor_reduce(out=o, in_=i, axis=AX.X, op=ALU.add), R=[tmB], W=[haB])
        nseg = 2 if last else 4
        zp = self.sbn("zp", [128, 2052], F32, 2)
        gbt = self.sbn("gbt", [128, 2048], BF16, 2)
        y = self.sbn("ycv", [128, 2048], F32, 2)
        yo = self.sbn("yo", [128, 2048], BF16, 2)
        zl = self.sbn("zl", [128, 2048], BF16, 2)
        it = 0
        for c in range(2):
            for seg in range(nseg):
                if seg < 2:
                    t0, n = seg * LAT, LAT
                else:
                    t0, n = 4096 + (seg - 2) * 256, 256
                z, zB = zp[it % 2]
                zb, zbB = zl[it % 2]
                g, gB = gbt[it % 2]
                yy, yB = y[it % 2]
                o, oB = yo[it % 2]
                it += 1
                self.dma("sync", zb[:, 0:n], S["z"][c, :, t0:t0 + n], zbB, W=[zbB])
                self.dma("gpsimd", g[:, 0:n], S["gb"][c, :, t0:t0 + n], gB, W=[gB])
                self.cp("gpsimd", z[:, 1:n + 1], zb[:, 0:n], R=[zbB], W=[zB])
                if seg < 2:
                    hi = c * 4 + seg * 2
                    self.cp("vector", z[:, 0:1], halo[:, hi:hi + 1], R=[haB], W=[zB])
                    self.cp("vector", z[:, n + 1:n + 2], halo[:, hi + 1:hi + 2], R=[haB], W=[zB])
                else:
                    self.t.op("vector", lambda e, a=z[:, 0:1]: e.memset(a, 0.0), W=[zB])
                    self.t.op("vector", lambda e, a=z[:, n + 1:n + 2]: e.memset(a, 0.0), W=[zB])
                self.ts("vector", yy[:, 0:n], z[:, 1:n + 1], cw[:, c, 1:2], None, ALU.mult, None, R=[zB, cwB], W=[yB])
                self.stt(yy[:, 0:n], z[:, 0:n], cw[:, c, 0:1], yy[:, 0:n], ALU.mult, ALU.add, R=[zB, cwB, yB], W=[yB])
                self.stt(yy[:, 0:n], z[:, 2:n + 2], cw[:, c, 2:3], yy[:, 0:n], ALU.mult, ALU.add, R=[zB, cwB, yB], W=[yB])
                self.stt(o[:, 0:n], yy[:, 0:n], cb[:, c:c + 1], g[:, 0:n], ALU.add, ALU.mult, R=[yB, cwB, gB], W=[oB])
                self.dma("sync", S["mixT"][4 + c, :, t0:t0 + n], o[:, 0:n], oB, R=[oB])

    def phaseF(self, l, last):
        I, S = self.I, self.S
        self.gather((("eKT", "gKT"), ("eV", "gV")))
        Ec = self.sb("Ec", [128, 128, 128], BF16)
        Es = self.sb("Es", [128, 128, 128], BF16)
        cB = [Buf() for _ in range(4)]
        for qd in range(4):
            self.dma("sync", Ec[:, qd * 32:(qd + 1) * 32, :], I["Ec"][:, qd * 32:(qd + 1) * 32, :], cB[qd], W=[cB[qd]])
            self.dma("sync", Es[:, qd * 32:(qd + 1) * 32, :], I["Es"][:, qd * 32:(qd + 1) * 32, :], cB[qd], W=[cB[qd]])
        gF = S["gF"].rearrange("(r b i n) c -> r b i n c", r=8, b=2, i=16)
        NCH = 16
        f1 = self.sbn("f1", [128, NCH, 256], BF16, 2)
        a1 = self.sbn("a1s", [128, NCH, 512], BF16, 2)
        it = 0
        for b in range(2):
            for nch in range(128 // NCH):
                f, fB = f1[it % 2]
                a, aB = a1[it % 2]
                it += 1
                for r in range(8):
                    self.dma(("sync", "gpsimd")[r % 2], f[r * 16:(r + 1) * 16, :, :], gF[r, b, :, nch * NCH:(nch + 1) * NCH, :], fB, W=[fB])
                for q in range(NCH // 2):
                    pr, prB = self.ps(q % 2, 0)
                    pi, piB = self.ps(q % 2, 1)
                    for u in range(2):
                        n2l = q * 2 + u
                        n2 = nch * NCH + n2l
                        self.mm(pr[:, u * 256:(u + 1) * 256], Ec[:, n2, :], f[:, n2l, :], True, True, R=[cB[n2 // 32], fB], W=[prB])
                        self.mm(pi[:, u * 256:(u + 1) * 256], Es[:, n2, :], f[:, n2l, :], True, True, R=[cB[n2 // 32], fB], W=[piB])
                    av = a[:, q * 2:q * 2 + 2, :].rearrange("p u (c h) -> p u c h", c=2)
                    self.cp("vector", av[:, :, 0, :], pr.rearrange("p (u h) -> p u h", u=2), R=[prB], W=[aB])
                    self.act(av[:, :, 1, :], pi.rearrange("p (u h) -> p u h", u=2), AF.Copy, R=[piB], W=[aB])
                self.dma("sync", S["a1"][b, :, nch * NCH:(nch + 1) * NCH, :], a[:], aB, R=[aB])

    def phaseF2(self, l, last):
        I, S = self.I, self.S
        cB = Buf()
        T2a = self.sb("T2a", [128, 32], BF16)
        T2b = self.sb("T2b", [128, 32], BF16)
        CD = self.sb("CD", [128, 128], BF16)
        nSD = self.sb("nSD", [128, 128], BF16)
        C256 = self.sb("C256", [128, 2, 256], BF16)
        S256 = self.sb("S256", [128, 2, 256], BF16)
        for t_, n_ in ((T2a, "T2a"), (T2b, "T2b"), (CD, "CD"), (nSD, "nSD"), (C256, "C256"), (S256, "S256")):
            self.dma("sync", t_[:], I[n_], cB, W=[cB])
        a2 = self.sbn("a2s", [128, 16, 512], BF16, 2)
        zt = self.sbn("zt", [128, 2, 2, 2048], BF16, 1)
        yo = self.sbn("yof", [128, 512], BF16, 2)
        it = 0
        for b in range(2):
            z, zB = zt[0]
            for kc in range(8):
                a, aB = a2[kc % 2]
                self.dma(("sync", "gpsimd")[kc % 2], a[:], S["a1"][b, kc * 16:(kc + 1) * 16, :, :].rearrange("k n x -> n k x"), aB, W=[aB])
                for half in range(2):
                    pt, pB = self.ps(kc % 2, half)
                    for k1 in range(16):
                        o = pt[:, k1 * 32:(k1 + 1) * 32]
                        self.mm(o, a[:, k1, half * 128:(half + 1) * 128], T2a[:], True, False, R=[aB, cB], W=[pB])
                        self.mm(o, a[:, k1, 256 + half * 128:256 + (half + 1) * 128], T2b[:], False, True, R=[aB, cB], W=[pB])
                    for c in range(2):
                        dst = z[:, half, c, :].rearrange("p (k2 k1) -> p k2 k1", k1=128)[:, :, kc * 16:(kc + 1) * 16]
                        srcv = pt.rearrange("p (k1 c k2) -> p c k2 k1", k1=16, c=2)[:, c]
                        self.cp(("vector", "gpsimd")[0], dst, srcv, R=[pB], W=[zB])
            for half in range(2):
                for tq in range(4):
                    pt, pB = self.ps(2 + (it % 2), 0)
                    self.mm(pt, CD[:], z[:, half, 0, tq * 512:(tq + 1) * 512], True, False, R=[cB, zB], W=[pB])
                    self.mm(pt, nSD[:], z[:, half, 1, tq * 512:(tq + 1) * 512], False, True, R=[cB, zB], W=[pB])
                    o, oB = yo[it % 2]
                    it += 1
                    self.act(o[:], pt, AF.Copy, R=[pB], W=[oB], scale=1.0 / 1024.0)
                    t0 = b * LAT + tq * 512
                    self.dma("sync", S["mixT"][6 + half, :, t0:t0 + 512], o[:], oB, R=[oB])
        if not last:
            fc = self.sb("fcs", [128, 4, 256], BF16)
            fcB = Buf()
            self.dma("sync", fc[:], S["fc"].rearrange("(j p) e -> p j e", p=128), fcB, W=[fcB])
            zc = self.sbn("zc", [128, 2, 256], BF16, 2)
            for b in range(2):
                for half in range(2):
                    z, zB = zc[half]
                    pr, prB = self.ps(0, 0)
                    pi, piB = self.ps(0, 1)
                    for j in range(2):
                        self.mm(pr[:, 0:256], fc[:, b * 2 + j, half * 128:(half + 1) * 128], C256[:, j, :], j == 0, j == 1, R=[fcB, cB], W=[prB])
                    for j in range(2):
                        self.mm(pi[:, 0:256], fc[:, b * 2 + j, half * 128:(half + 1) * 128], S256[:, j, :], j == 0, j == 1, R=[fcB, cB], W=[piB])
                    self.cp("vector", z[:, 0, :], pr[:, 0:256], R=[prB], W=[zB])
                    self.cp("vector", z[:, 1, :], pi[:, 0:256], R=[piB], W=[zB])
                    pt, pB = self.ps(1, 0)
                    self.mm(pt[:, 0:256], CD[:], z[:, 0, :], True, False, R=[cB, zB], W=[pB])
                    self.mm(pt[:, 0:256], nSD[:], z[:, 1, :], False, True, R=[cB, zB], W=[pB])
                    o, oB = yo[it % 2]
                    it += 1
                    self.act(o[:, 0:256], pt[:, 0:256], AF.Copy, R=[pB], W=[oB], scale=1.0 / 128.0)
                    t0 = 4096 + b * 256
                    self.dma("sync", S["mixT"][6 + half, :, t0:t0 + 256], o[:, 0:256], oB, R=[oB])

    def phaseB(self, l, last):
        I, S = self.I, self.S
        lam_init = 0.8 - 0.6 * math.exp(-0.3 * l)
        dl = self.sb("dl", [128, 256])
        dlB = Buf()
        self.dma("sync", dl[:], I["diff_lambda"][l, :].partition_broadcast(128), dlB, W=[dlB])
        pr = self.sb("lprod", [128, 2, 64])
        prB = Buf()
        self.tt("vector", pr[:, 0, :], dl[:, 0:64], dl[:, 64:128], ALU.mult, R=[dlB], W=[prB])
        self.tt("vector", pr[:, 1, :], dl[:, 128:192], dl[:, 192:256], ALU.mult, R=[dlB], W=[prB])
        ls = self.sb("lsum", [128, 2])
        lsB = Buf()
        self.t.op("vector", lambda e: e.tensor_reduce(out=ls[:], in_=pr[:], axis=AX.X, op=ALU.add), R=[prB], W=[lsB])
        le = self.sb("lexp", [128, 2])
        leB = Buf()
        self.act(le[:], ls[:], AF.Exp, R=[lsB], W=[leB])
        nlam = self.sb("nlam", [128, 1])
        nlB = Buf()
        self.tt("vector", nlam[:], le[:, 1:2], le[:, 0:1], ALU.subtract, R=[leB], W=[nlB])
        self.ts("vector", nlam[:], nlam[:], -lam_init, None, ALU.add, None, R=[nlB], W=[nlB])
        swl = self.sb("swl", [128, 1])
        swB = Buf()
        self.dma("sync", swl[:], I["subln_w"][l, :].unsqueeze(1), swB, W=[swB], slow=True)
        self.ts("vector", swl[:], swl[:], 1.0 - lam_init, None, ALU.mult, None, R=[swB], W=[swB])

        KT = self.sbn("KT", [128, NKT * 128], BF16, 2)
        V = self.sbn("V", [128, NKT, 128], BF16, 2)
        QT = self.sbn("QT", [128, LAT + 256], BF16, 2)
        E = self.sbn("E", [128, 1024], BF16, 6)
        racc = self.sbn("racc", [128, 1024], F32, 2)
        racc2 = self.sbn("racc2", [128, 1024], F32, 2)
        lns = self.sbn("lns", [128, 1024], F32, 1)
        rinv = self.sbn("rinv", [128, 1024], F32, 1)
        o1 = self.sbn("o1", [128, 512], F32, 2)
        o2 = self.sbn("o2", [128, 512], F32, 1)
        sq = self.sbn("sq", [128, 512], F32, 1)
        rstd = self.sbn("rstd", [128, 512], F32, 1)
        ao = self.sbn("ao", [128, 512], BF16, 2)
        gKT = S["gKT"].rearrange("(r b h d) t -> r b h d t", r=8, b=2, h=4)
        gV = S["gV"].rearrange("(r b j p) e -> r b p j e", r=8, b=2, p=128)
        vc = S["vc"].rearrange("(b j p) e -> b p j e", b=2, p=128)
        heads = [(b, h) for b in range(2) for h in range(4)]

        def load(i):
            b, h = heads[i]
            kt, ktB = KT[i % 2]
            v, vB = V[i % 2]
            qt, qtB = QT[i % 2]
            self.dma("sync", qt[:, 0:LAT], S["qT"][h, :, b * LAT:(b + 1) * LAT], qtB, W=[qtB])
            self.dma("sync", qt[:, LAT:LAT + 256], S["qT"][h, :, 4096 + b * 256:4096 + (b + 1) * 256], qtB, W=[qtB])
            self.dma("sync", kt[:, 0:256], S["kTc"][b, h], ktB, W=[ktB])
            self.dma("gpsimd", v[:, 0:2, :], vc[b, :, :, h * 128:(h + 1) * 128], vB, W=[vB])
            for r in range(8):
                self.dma(("sync", "gpsimd")[r % 2], kt[:, 256 + r * LAT:256 + (r + 1) * LAT], gKT[r, b, h], ktB, W=[ktB])
                self.dma(("gpsimd", "sync")[r % 2], v[:, 2 + r * 16:2 + (r + 1) * 16, :], gV[r, b, :, :, h * 128:(h + 1) * 128], vB, W=[vB])

        nqb = 4 if last else 5
        blocks = []
        for i, (b, h) in enumerate(heads):
            for qb in range(nqb):
                if qb < 4:
                    blocks.append(dict(i=i, b=b, h=h, q0=qb * 512, nq=512, nkt=NKT, t0=b * LAT + qb * 512, first=(qb == 0)))
                else:
                    blocks.append(dict(i=i, b=b, h=h, q0=LAT, nq=256, nkt=2, t0=4096 + b * 256, first=False))
        its = []
        for bi, bk in enumerate(blocks):
            for t in range(bk["nkt"]):
                its.append((bi, t))
        st = {"e": {}, "grp": [], "gi": 0, "first": True}
        ptmp = self.sbn("ptmp", [128, 1024], BF16, 2)
        ptmp2 = self.sbn("ptmp2", [128, 1024], BF16, 2)

        def front(g):
            bi, t = its[g]
            bk = blocks[bi]
            i = bk["i"]
            kt, ktB = KT[i % 2]
            qt, qtB = QT[i % 2]
            nq, q0 = bk["nq"], bk["q0"]
            si = g % 2
            for p in range(2):
                pt, pB = self.ps(si, p)
                self.mm(pt[:, 0:nq], kt[p * 64:(p + 1) * 64, t * 128:(t + 1) * 128], qt[p * 64:(p + 1) * 64, q0:q0 + nq], True, True, R=[ktB, qtB], W=[pB])
            e_, eB = E[g % 6]
            if nq == 512:
                self.act(e_[:], self.PSB[si][:, :], AF.Exp, R=[self.PB[si][0], self.PB[si][1]], W=[eB], scale=0.125)
            else:
                self.act(e_[:].rearrange("p (m q) -> p m q", m=2)[:, :, 0:nq], self.PSB[si][:, :].rearrange("p (m q) -> p m q", m=2)[:, :, 0:nq], AF.Exp, R=[self.PB[si][0], self.PB[si][1]], W=[eB], scale=0.125)
            ra, raB = racc[bi % 2]
            grp = st["grp"]
            grp.append((e_, eB))
            lastt = (t == bk["nkt"] - 1)
            pa, paB = ptmp[(st["gi"]) % 2]
            pb, pbB = ptmp2[(st["gi"]) % 2]
            if len(grp) == 2:
                self.tt("vector", pa[:], grp[0][0][:], grp[1][0][:], ALU.add, R=[grp[0][1], grp[1][1]], W=[paB])
            if len(grp) == 4:
                self.tt("vector", pb[:], grp[2][0][:], grp[3][0][:], ALU.add, R=[grp[2][1], grp[3][1]], W=[pbB])
                self.tt("vector", pa[:], pa[:], pb[:], ALU.add, R=[paB, pbB], W=[paB])
            if len(grp) == 4 or lastt:
                if len(grp) == 1:
                    srcap, srcB = grp[0][0][:], grp[0][1]
                elif len(grp) == 3:
                    self.tt("vector", pa[:], pa[:], grp[2][0][:], ALU.add, R=[paB, grp[2][1]], W=[paB])
                    srcap, srcB = pa[:], paB
                else:
                    srcap, srcB = pa[:], paB
                if st["first"]:
                    self.cp("vector", ra[:], srcap, R=[srcB], W=[raB])
                else:
                    self.tt("vector", ra[:], ra[:], srcap, ALU.add, R=[srcB, raB], W=[raB])
                st["first"] = lastt
                st["grp"] = []
                st["gi"] += 1

        def back(g):
            bi, t = its[g]
            bk = blocks[bi]
            i = bk["i"]
            v, vB = V[i % 2]
            nq, nkt = bk["nq"], bk["nkt"]
            e_, eB = E[g % 6]
            poA, poAB = self.ps(2, 0)
            poB, poBB = self.ps(2, 1)
            self.mm(poA[:, 0:nq], v[:, t, :], e_[:, 0:nq], t == 0, t == nkt - 1, R=[vB, eB], W=[poAB])
            self.mm(poB[:, 0:nq], v[:, t, :], e_[:, 512:512 + nq], t == 0, t == nkt - 1, R=[vB, eB], W=[poBB])
            if t == nkt - 1:
                if st["pend"] is not None:
                    epiB(st["pend"][0])
                epiA(bi)
                st["pend"] = [bi, 3]

        def epiA(bi):
            bk = blocks[bi]
            nq = bk["nq"]
            ra, raB = racc[bi % 2]
            poA, poAB = self.ps(2, 0)
            poB, poBB = self.ps(2, 1)
            sA, sAB = self.ps(3, 0)
            sBp, sBB = self.ps(3, 1)
            rb, rbB = racc2[bi % 2]
            two = False
            self.mm(sA[:, 0:nq], self.ones[:], ra[:, 0:nq], True, not two, R=[self.onesB, raB], W=[sAB])
            if two:
                self.mm(sA[:, 0:nq], self.ones[:], rb[:, 0:nq], False, True, R=[self.onesB, rbB], W=[sAB])
            self.mm(sBp[:, 0:nq], self.ones[:], ra[:, 512:512 + nq], True, not two, R=[self.onesB, raB], W=[sBB])
            if two:
                self.mm(sBp[:, 0:nq], self.ones[:], rb[:, 512:512 + nq], False, True, R=[self.onesB, rbB], W=[sBB])
            ln_, lnB = lns[0]
            ri, riB = rinv[0]
            self.act(ln_[:, 0:nq], sA[:, 0:nq], AF.Ln, R=[sAB], W=[lnB])
            self.act(ln_[:, 512:512 + nq], sBp[:, 0:nq], AF.Ln, R=[sBB], W=[lnB])
            self.act(ri[:], ln_[:], AF.Exp, R=[lnB], W=[riB], scale=-1.0)
            a1_, a1B = o1[bi % 2]
            a2_, a2B = o2[0]
            self.tt("vector", a1_[:, 0:nq], poA[:, 0:nq], ri[:, 0:nq], ALU.mult, R=[poAB, riB], W=[a1B])
            self.tt("vector", a2_[:, 0:nq], poB[:, 0:nq], ri[:, 512:512 + nq], ALU.mult, R=[poBB, riB], W=[a2B])
            self.stt(a1_[:, 0:nq], a2_[:, 0:nq], nlam[:, 0:1], a1_[:, 0:nq], ALU.mult, ALU.add, R=[a2B, nlB, a1B], W=[a1B])
            s_, sB_ = sq[0]
            self.tt("gpsimd", s_[:, 0:nq], a1_[:, 0:nq], a1_[:, 0:nq], ALU.mult, R=[a1B], W=[sB_])

        def epiB(bi):
            bk = blocks[bi]
            nq = bk["nq"]
            sA, sAB = self.ps(3, 0)
            s_, sB_ = sq[0]
            a1_, a1B = o1[bi % 2]
            self.mm(sA[:, 0:nq], self.ones128[:], s_[:, 0:nq], True, True, R=[self.onesB, sB_], W=[sAB])
            rs_, rsB = rstd[0]
            self.act(rs_[:, 0:nq], sA[:, 0:nq], AF.Ln, R=[sAB, self.onesB], W=[rsB], bias=self.epsT[:, 0:1])
            self.act(rs_[:, 0:nq], rs_[:, 0:nq], AF.Exp, R=[rsB], W=[rsB], scale=-0.5)
            self.tt("vector", a1_[:, 0:nq], a1_[:, 0:nq], rs_[:, 0:nq], ALU.mult, R=[rsB, a1B], W=[a1B])
            ao_, aoB = ao[bi % 2]
            self.ts("gpsimd", ao_[:, 0:nq], a1_[:, 0:nq], swl[:, 0:1], None, ALU.mult, None, R=[a1B, swB], W=[aoB])
            self.dma("sync", S["mixT"][bk["h"], :, bk["t0"]:bk["t0"] + nq], ao_[:, 0:nq], aoB, R=[aoB])

        st["pend"] = None
        load(0)
        n = len(its)
        for g in range(n + 1):
            if g < n:
                front(g)
            if g >= 1:
                back(g - 1)
            if g < n:
                bi_, t_ = its[g]
                bk_ = blocks[bi_]
                if t_ == 0 and bk_["q0"] == 1024 and bk_["i"] + 1 < len(heads):
                    load(bk_["i"] + 1)
            if st["pend"] is not None:
                st["pend"][1] -= 1
                if st["pend"][1] <= 0:
                    epiB(st["pend"][0])
                    st["pend"] = None
        if st["pend"] is not None:
            epiB(st["pend"][0])
            st["pend"] = None

    def ln_tiles(self, l, which):
        I = self.I
        g = self.sb("lng", [128, D])
        bt = self.sb("lnb", [128, D])
        B = Buf()
        self.dma("sync", g[:], I["ln%d_g" % which][l, :].partition_broadcast(128), B, W=[B])
        self.dma("gpsimd", bt[:], I["ln%d_b" % which][l, :].partition_broadcast(128), B, W=[B])
        gt = []
        for s in range(3):
            t_ = self.sb("gate", [128, D])
            self.dma(("sync", "gpsimd")[s % 2], t_[:], self.S["gates"][l, which - 1, s, :].partition_broadcast(128), B, W=[B])
            gt.append(t_)
        W_ = {"g": g, "b": bt, "gate": gt, "B": B}
        W_["t"] = self.sbn("lnt", [128, D], F32, 2)
        W_["st"] = self.sbn("lnst", [128, 12], F32, 2)
        W_["mv"] = self.sbn("lnmv", [128, 2], F32, 2)
        W_["rs"] = self.sbn("lnrs", [128, 1], F32, 2)
        W_["nm"] = self.sbn("lnnm", [128, 1], F32, 2)
        W_["o"] = self.sbn("lno", [128, D], F32, 2)
        W_["i"] = 0
        return W_

    def ln_epi(self, W_, s, p0, p0B, p1, p1B, x, xB, dst):
        i = W_["i"]
        W_["i"] += 1
        t, tB = W_["t"][i % 2]
        st, stB = W_["st"][i % 2]
        mv, mvB = W_["mv"][i % 2]
        rs, rsB = W_["rs"][i % 2]
        o, oB = W_["o"][i % 2]
        g = W_["gate"][s]
        B = W_["B"]
        self.tt("vector", t[:, 0:512], p0, g[:, 0:512], ALU.mult, R=[p0B, B], W=[tB])
        self.tt("vector", t[:, 512:1024], p1, g[:, 512:1024], ALU.mult, R=[p1B, B], W=[tB])
        self.stt(t[:], x, ALPHA, t[:], ALU.mult, ALU.add, R=[xB, tB], W=[tB])
        self.t.op("vector", lambda e, o_=st[:, 0:6], i_=t[:, 0:512]: e.bn_stats(out=o_, in_=i_), R=[tB], W=[stB])
        self.t.op("vector", lambda e, o_=st[:, 6:12], i_=t[:, 512:1024]: e.bn_stats(out=o_, in_=i_), R=[tB], W=[stB])
        self.t.op("vector", lambda e, o_=mv[:], i_=st[:]: e.bn_aggr(out=o_, in_=i_), R=[stB], W=[mvB])
        self.act(rs[:], mv[:, 1:2], AF.Sqrt, R=[mvB, self.onesB], W=[rsB], bias=self.epsT[:, 0:1])
        self.t.op("vector", lambda e, o=rs[:]: e.reciprocal(out=o, in_=o), R=[rsB], W=[rsB])
        nm, nmB = W_["nm"][i % 2]
        self.ts("vector", nm[:], mv[:, 0:1], rs[:, 0:1], -1.0, ALU.mult, ALU.mult, R=[mvB, rsB], W=[nmB])
        self.act(o[:], t[:], AF.Identity, R=[tB, rsB, nmB], W=[oB], bias=nm[:, 0:1], scale=rs[:, 0:1])
        self.tt("vector", o[:], o[:], W_["g"][:], ALU.mult, R=[oB, B], W=[oB])
        self.tt("gpsimd", o[:], o[:], W_["b"][:], ALU.add, R=[oB, B], W=[oB])
        self.dma("sync", dst, o[:], oB, R=[oB])

    def phaseD(self, l, last):
        I, S = self.I, self.S
        wo = self.sb("wo", [128, 8, D], BF16)
        woB = [Buf() for _ in range(8)]
        self.load_w(wo, woB, I["w_out"][l], 8, D, D)
        W_ = self.ln_tiles(l, 1)
        mx = self.sbn("mx", [128, 8, 512], BF16, 2)
        xt = self.sbn("xtd", [128, D], F32, 2)
        src = I["xin"] if l == 0 else S["xres"]
        nb = 8 if last else 9
        xi = 0
        for blk in range(nb):
            s = 0 if blk < 4 else (1 if blk < 8 else 2)
            tok0 = blk * 512
            m, mB = mx[blk % 2]
            self.dma("gpsimd", m[:], S["mixT"][:, :, tok0:tok0 + 512].rearrange("k p t -> p k t"), mB, W=[mB])
            for j in range(4):
                x, xB = xt[xi % 2]
                p0, p0B = self.ps(xi % 2, 0)
                p1, p1B = self.ps(xi % 2, 1)
                xi += 1
                r0 = tok0 + j * 128
                self.dma("sync", x[:], src[r0:r0 + 128, :], xB, W=[xB])
                for k in range(8):
                    self.mm(p0, m[:, k, j * 128:(j + 1) * 128], wo[:, k, 0:512], k == 0, k == 7, R=[mB, woB[k]], W=[p0B])
                for k in range(8):
                    self.mm(p1, m[:, k, j * 128:(j + 1) * 128], wo[:, k, 512:1024], k == 0, k == 7, R=[mB, woB[k]], W=[p1B])
                self.ln_epi(W_, s, p0, p0B, p1, p1B, x[:], xB, S["x1"][r0:r0 + 128, :])

    def phaseE1(self, l, last):
        I, S = self.I, self.S
        wu = self.sb("wu", [128, 8, 4 * D], BF16)
        wuB = [Buf() for _ in range(8)]
        self.load_w(wu, wuB, I["w_up"][l], 8, 4 * D, 2048)
        xt = self.sbn("xte", [128, 4, D], F32, 2)
        xmT = self.sbn("xm2T", [128, 8, 512], BF16, 2)
        rl = self.sbn("rl", [128, 512], F32, 3)
        ho = self.sbn("ho", [128, 4, 512], BF16, 2)
        nb = 8 if last else 9
        ri = 0
        for blk in range(nb):
            s = 0 if blk < 4 else (1 if blk < 8 else 2)
            tok0 = blk * 512
            x, xB = xt[blk % 2]
            self.dma("sync", x[:], S["x1"][tok0:tok0 + 512, :].rearrange("(j p) d -> p j d", p=128), xB, W=[xB])
            xm, xmB = xmT[blk % 2]
            for k in range(8):
                pt, pB = self.ps(0, k % 2)
                for j in range(4):
                    self.tr(pt[:, j * 128:(j + 1) * 128], x[:, j, k * 128:(k + 1) * 128], self.ident[:], R=[xB, self.identB], W=[pB])
                self.act(xm[:, k, :], pt, AF.Identity, R=[pB, self.modB], W=[xmB], bias=self.mod(l, 3, k, s), scale=self.mod(l, 4, k, s))
            for fq in range(8):
                h_, hB = ho[fq % 2]
                for u in range(4):
                    fc = fq * 4 + u
                    pt, pB = self.ps(1 + (fc % 4) // 2, fc % 2)
                    for k in range(8):
                        self.mm(pt, wu[:, k, fc * 128:(fc + 1) * 128], xm[:, k, :], k == 0, k == 7, R=[wuB[k], xmB], W=[pB])
                    r_, rB = rl[ri % 3]
                    ri += 1
                    self.act(r_[:], pt, AF.Relu, R=[pB], W=[rB])
                    self.tt(("vector", "gpsimd")[fc % 2], h_[:, u, :], r_[:], r_[:], ALU.mult, R=[rB], W=[hB])
                self.dma("gpsimd", S["hT"][blk, :, fq * 4:(fq + 1) * 4, :], h_[:], hB, R=[hB])

    def phaseE2(self, l, last, final):
        I, S = self.I, self.S
        wd = self.sb("wd", [128, 32, D], BF16)
        wdB = [Buf() for _ in range(32)]
        self.load_w(wd, wdB, I["w_down"][l], 32, D, D)
        W_ = self.ln_tiles(l, 2)
        hx = self.sbn("hx", [128, 32, 512], BF16, 2)
        xt = self.sbn("xt2", [128, D], F32, 2)
        nb = 8 if (last or final) else 9
        xi = 0
        for blk in range(nb):
            s = 0 if blk < 4 else (1 if blk < 8 else 2)
            hh, hB = hx[blk % 2]
            self.dma("gpsimd", hh[:], S["hT"][blk], hB, W=[hB])
            for j in range(4):
                r0 = blk * 512 + j * 128
                x, xB = xt[xi % 2]
                p0, p0B = self.ps(xi % 2, 0)
                p1, p1B = self.ps(xi % 2, 1)
                xi += 1
                self.dma("sync", x[:], S["x1"][r0:r0 + 128, :], xB, W=[xB])
                for k in range(32):
                    self.mm(p0, hh[:, k, j * 128:(j + 1) * 128], wd[:, k, 0:512], k == 0, k == 31, R=[hB, wdB[k]], W=[p0B])
                for k in range(32):
                    self.mm(p1, hh[:, k, j * 128:(j + 1) * 128], wd[:, k, 512:1024], k == 0, k == 31, R=[hB, wdB[k]], W=[p1B])
                if final:
                    dst = self.yout[r0:r0 + 128, :]
                else:
                    dst = S["xres"][r0:r0 + 128, :]
                self.ln_epi(W_, s, p0, p0B, p1, p1B, x[:], xB, dst)


def _bf(a):
    return np.ascontiguousarray(a.astype(ml_dtypes.bfloat16))


def _shared_consts():
    C = {}
    C["ident"] = np.eye(128, dtype=np.float32)
    n1 = np.arange(128)[:, None, None].astype(np.int64)
    n2 = np.arange(128)[None, :, None].astype(np.int64)
    k1 = np.arange(128)[None, None, :].astype(np.int64)
    ph = ((k1 * (128 * n1 + n2)) % SEQ).astype(np.float64) * (2 * np.pi / SEQ)
    C["Ec"] = _bf(np.cos(ph))
    C["Es"] = _bf(np.sin(ph))
    j = np.arange(64)[:, None]
    m = np.arange(64)[None, :]
    c64 = np.cos(2 * np.pi * j * m / 64.0)
    s64 = np.sin(2 * np.pi * j * m / 64.0)
    CD = np.zeros((128, 128))
    SD = np.zeros((128, 128))
    for g in range(2):
        CD[g * 64:(g + 1) * 64, g * 64:(g + 1) * 64] = c64
        SD[g * 64:(g + 1) * 64, g * 64:(g + 1) * 64] = s64
    C["CD"] = _bf(CD)
    C["nSD"] = _bf(-SD)
    n = np.arange(256)[:, None]
    k = np.arange(256)[None, :]
    ph = ((n * k) % 256) * (2 * np.pi / 256.0)
    C["C256"] = _bf(np.cos(ph).reshape(2, 128, 256).transpose(1, 0, 2))
    C["S256"] = _bf(np.sin(ph).reshape(2, 128, 256).transpose(1, 0, 2))
    return C


def _core_consts(core):
    C = {}
    inv = (1.0 / (10000.0 ** (np.arange(0, 32, 2, dtype=np.float32) / np.float32(32)))).astype(np.float32)
    n = (LAT * core + np.arange(LAT)).astype(np.int64)
    rows = (n // 64).astype(np.float32)
    cols = (n % 64).astype(np.float32)
    cosT = np.ones((64, NTOK), np.float32)
    sinT = np.zeros((64, NTOK), np.float32)
    for d in range(64):
        seg, i = d // 16, d % 16
        pos = rows if seg < 2 else cols
        ang = (pos * inv[i]).astype(np.float32)
        c = np.cos(ang).astype(np.float32)
        s = np.sin(ang).astype(np.float32)
        sg = -1.0 if seg % 2 == 0 else 1.0
        for b in range(2):
            cosT[d, b * LAT:(b + 1) * LAT] = c
            sinT[d, b * LAT:(b + 1) * LAT] = sg * s
    C["ropeC"] = np.ascontiguousarray(np.concatenate([cosT, cosT], 0))
    C["ropeS"] = np.ascontiguousarray(np.concatenate([sinT, sinT], 0))
    n2 = np.arange(128)[:, None]
    k2 = (16 * core + np.arange(16))[None, :]
    ph = ((n2 * k2) % 128) * (2 * np.pi / 128.0)
    c2, s2 = np.cos(ph), np.sin(ph)
    C["T2a"] = _bf(np.concatenate([c2, s2], 1))
    C["T2b"] = _bf(np.concatenate([-s2, c2], 1))
    sl = np.zeros((128, 8), np.float32)
    sr = np.zeros((128, 8), np.float32)
    if core >= 1:
        sl[:, core - 1] = 1.0
    if core <= 6:
        sr[:, core + 1] = 1.0
    C["selL"] = sl
    C["selR"] = sr
    return C


def _partner_perm():
    p = np.arange(512)
    d = p % 64
    seg = d // 16
    pd = np.where(seg % 2 == 0, d + 16, d - 16)
    return (p // 64) * 64 + pd


_NC_CACHE = {}


def _prep_inputs(x, c, ctx, c_ctx, w_mod, b_mod, w_in, diff_lambda, subln_w, conv_w, conv_b,
                 w_out, ln1_g, ln1_b, w_up, w_down, ln2_g, ln2_b):
    f = lambda a: np.ascontiguousarray(np.asarray(a, dtype=np.float32))
    x, c, ctx, c_ctx, w_in = f(x), f(c), f(ctx), f(c_ctx), f(w_in)
    perm = _partner_perm()
    q, k = w_in[:, :, 0:512], w_in[:, :, 512:1024]
    v = w_in[:, :, 1024:1536]
    ugg = w_in[:, :, 1536:2304]
    fo = w_in[:, :, 2304:2560]
    w_fm = np.ascontiguousarray(np.concatenate([q, k, ugg, q[:, :, perm], k[:, :, perm]], axis=2))
    w_tm = np.ascontiguousarray(np.concatenate([v, fo], axis=2))
    cc = np.stack([c[0], c[1], c_ctx], 0)
    ccT = np.ascontiguousarray(cc.reshape(3, 8, 128).transpose(2, 1, 0))
    shared = {
        "ccT": ccT, "w_mod": f(w_mod), "b_mod": f(b_mod), "w_in_fm": w_fm, "w_in_tm": w_tm,
        "diff_lambda": f(diff_lambda).reshape(NL, 256), "subln_w": f(subln_w), "conv_w": f(conv_w),
        "conv_b": f(conv_b), "w_out": f(w_out), "ln1_g": f(ln1_g), "ln1_b": f(ln1_b), "w_up": f(w_up),
        "w_down": f(w_down), "ln2_g": f(ln2_g), "ln2_b": f(ln2_b),
    }
    shared.update(_shared_consts())
    maps = []
    for core in range(NCORE):
        m = dict(shared)
        sl = slice(LAT * core, LAT * (core + 1))
        m["xin"] = np.ascontiguousarray(np.concatenate([x[0, sl], x[1, sl], ctx[0], ctx[1]], 0))
        m.update(_core_consts(core))
        maps.append(m)
    return maps


def kernel(**inputs):
    maps = _prep_inputs(**inputs)
    if "nc" not in _NC_CACHE:
        _NC_CACHE["nc"] = Builder().build()
    res = run_bass_kernel_spmd(_NC_CACHE["nc"], maps, core_ids=list(range(NCORE)))
    out = np.empty((2, SEQ, D), np.float32)
    for core in range(NCORE):
        y = res.results[core]["y"]
        out[0, LAT * core:LAT * (core + 1)] = y[0:LAT]
        out[1, LAT * core:LAT * (core + 1)] = y[LAT:2 * LAT]
    return out
```
